# Optimizing a Trainium2 kernel written in Bass

```python
import math
import jax, jax.numpy as jnp
from jax import lax
import numpy as np

D_MODEL = 2048
BATCH = 2
SEQ = 16384
DEPTH = 1

CHUNK = 64
Q_BLOCK = 128
ROPE_THETA = 10000.0
NORM_EPS = 1e-6

N_HEADS = 16
N_KV_HEADS = 4
HEAD_DIM = 128
ATTN_WIDTH = N_HEADS * HEAD_DIM
KV_WIDTH = N_KV_HEADS * HEAD_DIM
TOPK_MAX = 256
N_IDX_HEADS = 16
IDX_DIM = 64

SSM_EXPAND = 2
SSM_WIDTH = SSM_EXPAND * D_MODEL
SSM_HEAD_DIM = 64
SSM_HEADS = SSM_WIDTH // SSM_HEAD_DIM
SSM_GROUPS = 8
SSM_STATE = 128
CONV_WIDTH = 4
CONV_CH = SSM_WIDTH + 2 * SSM_GROUPS * SSM_STATE

D_FF = -(-8 * D_MODEL // (3 * 256)) * 256

IN_SIZES = (ATTN_WIDTH, KV_WIDTH, KV_WIDTH, N_IDX_HEADS * IDX_DIM, IDX_DIM, N_IDX_HEADS, SSM_WIDTH, CONV_CH, SSM_HEADS)
IN_COLS = sum(IN_SIZES)

kernel_name = 'hybrid_dsa_ssd_block'


def rms_norm(x, g):
    xf = x.astype(jnp.float32)
    y = xf * lax.rsqrt(jnp.mean(xf * xf, axis=-1, keepdims=True) + NORM_EPS)
    return (y * g.astype(jnp.float32)).astype(x.dtype)


def rope_tables(seq, dim, dtype):
    pos = jnp.arange(seq, dtype=jnp.float32)
    inv = ROPE_THETA ** (-jnp.arange(0, dim, 2, dtype=jnp.float32) / dim)
    ang = pos[:, None] * inv[None, :]
    return jnp.cos(ang).astype(dtype), jnp.sin(ang).astype(dtype)


def apply_rope(x, cos, sin):
    x1, x2 = jnp.split(x, 2, axis=-1)
    c = cos[:, None, :]
    s = sin[:, None, :]
    return jnp.concatenate([x1 * c - x2 * s, x2 * c + x1 * s], axis=-1)


def dsa_attention(q, k, v, q_idx, k_idx, w_idx):
    bsz, seq = q.shape[0], q.shape[1]
    topk = min(TOPK_MAX, seq // 4)
    nblk = seq // Q_BLOCK
    grp = N_HEADS // N_KV_HEADS
    key_pos = jnp.arange(seq, dtype=jnp.int32)

    def to_blocks(t):
        return t.reshape(bsz, nblk, Q_BLOCK, *t.shape[2:]).swapaxes(0, 1)

    def block_fn(args):
        qb, qib, wb, start = args
        qpos = start + jnp.arange(Q_BLOCK, dtype=jnp.int32)
        limit = (qpos // CHUNK + 1) * CHUNK
        visible = key_pos[None, :] < limit[:, None]
        dots = jnp.einsum('bqhd,bsd->bqhs', qib, k_idx)
        iscore = jnp.einsum('bqh,bqhs->bqs', wb, jax.nn.relu(dots)).astype(jnp.float32)
        iscore = jnp.where(visible[None], iscore, -jnp.inf)
        _, sel = lax.top_k(iscore, topk)
        sel_ok = sel < limit[None, :, None]
        kg = jax.vmap(lambda kk, ii: kk[ii])(k, sel)
        vg = jax.vmap(lambda vv, ii: vv[ii])(v, sel)
        qg = qb.reshape(bsz, Q_BLOCK, N_KV_HEADS, grp, HEAD_DIM)
        logits = jnp.einsum('bqhgd,bqkhd->bqhgk', qg, kg).astype(jnp.float32) * (HEAD_DIM ** -0.5)
        logits = jnp.where(sel_ok[:, :, None, None, :], logits, -jnp.inf)
        p = jax.nn.softmax(logits, axis=-1).astype(v.dtype)
        o = jnp.einsum('bqhgk,bqkhd->bqhgd', p, vg)
        return o.reshape(bsz, Q_BLOCK, ATTN_WIDTH)

    starts = jnp.arange(nblk, dtype=jnp.int32) * Q_BLOCK
    out = lax.map(block_fn, (to_blocks(q), to_blocks(q_idx), to_blocks(w_idx), starts))
    return out.swapaxes(0, 1).reshape(bsz, seq, ATTN_WIDTH)


def causal_dwconv(x, w, b):
    out = lax.conv_general_dilated(x, w[:, None, :], window_strides=(1,), padding=[(CONV_WIDTH - 1, 0)],
                                   dimension_numbers=('NWC', 'WIO', 'NWC'), feature_group_count=x.shape[-1])
    return out + b


def ssd_scan(xs, dt, a_head, bm, cm):
    bsz, seq = xs.shape[0], xs.shape[1]
    nc = seq // CHUNK
    hg = SSM_HEADS // SSM_GROUPS
    f32 = jnp.float32
    xdt = (xs.astype(f32) * dt[..., None]).reshape(bsz, nc, CHUNK, SSM_GROUPS, hg, SSM_HEAD_DIM)
    a = (dt * a_head).reshape(bsz, nc, CHUNK, SSM_GROUPS, hg)
    bf = bm.astype(f32).reshape(bsz, nc, CHUNK, SSM_GROUPS, SSM_STATE)
    cf = cm.astype(f32).reshape(bsz, nc, CHUNK, SSM_GROUPS, SSM_STATE)
    seq_first = lambda t: jnp.moveaxis(t, 1, 0)
    causal = jnp.tril(jnp.ones((CHUNK, CHUNK), dtype=bool))

    def step(h, inp):
        xc, ac, bc, cc = inp
        acum = jnp.cumsum(ac, axis=1)
        diff = acum[:, :, None] - acum[:, None, :]
        decay = jnp.exp(jnp.where(causal[None, :, :, None, None], diff, -jnp.inf))
        cb = jnp.einsum('btgn,bsgn->btsg', cc, bc)
        y = jnp.einsum('btsg,btsgh,bsghp->btghp', cb, decay, xc)
        y = y + jnp.einsum('btgn,bghpn,btgh->btghp', cc, h, jnp.exp(acum))
        tail = jnp.exp(acum[:, -1:] - acum)
        h = h * jnp.exp(acum[:, -1])[..., None, None] + jnp.einsum('bsgn,bsgh,bsghp->bghpn', bc, tail, xc)
        return h, y

    h0 = jnp.zeros((bsz, SSM_GROUPS, hg, SSM_HEAD_DIM, SSM_STATE), f32)
    _, ys = lax.scan(step, h0, (seq_first(xdt), seq_first(a), seq_first(bf), seq_first(cf)))
    return jnp.moveaxis(ys, 0, 1).reshape(bsz, seq, SSM_HEADS, SSM_HEAD_DIM)


def mamba_branch(xbc_raw, z, dt_raw, conv_w, conv_b, dt_bias, a_log, d_skip, g_norm):
    bsz, seq = z.shape[0], z.shape[1]
    f32 = jnp.float32
    xbc = jax.nn.silu(causal_dwconv(xbc_raw, conv_w, conv_b))
    xs, bm, cm = jnp.split(xbc, [SSM_WIDTH, SSM_WIDTH + SSM_GROUPS * SSM_STATE], axis=-1)
    xs = xs.reshape(bsz, seq, SSM_HEADS, SSM_HEAD_DIM)
    bm = bm.reshape(bsz, seq, SSM_GROUPS, SSM_STATE)
    cm = cm.reshape(bsz, seq, SSM_GROUPS, SSM_STATE)
    dt = jax.nn.softplus(dt_raw.astype(f32) + dt_bias.astype(f32))
    a_head = -jnp.exp(a_log.astype(f32))
    y = ssd_scan(xs, dt, a_head, bm, cm) + d_skip.astype(f32)[:, None] * xs.astype(f32)
    y = y.reshape(bsz, seq, SSM_WIDTH) * jax.nn.silu(z.astype(f32))
    yg = y.reshape(bsz, seq, SSM_GROUPS, SSM_WIDTH // SSM_GROUPS)
    yg = yg * lax.rsqrt(jnp.mean(yg * yg, axis=-1, keepdims=True) + NORM_EPS)
    return (yg.reshape(bsz, seq, SSM_WIDTH) * g_norm.astype(f32)).astype(z.dtype)


def hybrid_mixer(u, w_in, w_gate, w_attn_branch, w_ssm_branch, w_out, conv_w, conv_b, dt_bias, a_log, d_skip, g_ssm_norm):
    bsz, seq = u.shape[0], u.shape[1]
    proj = u @ w_in
    cuts = list(np.cumsum(IN_SIZES)[:-1])
    q, k, v, q_idx, k_idx, w_idx, z, xbc, dt_raw = jnp.split(proj, cuts, axis=-1)
    cos_a, sin_a = rope_tables(seq, HEAD_DIM, u.dtype)
    cos_i, sin_i = rope_tables(seq, IDX_DIM, u.dtype)
    q = apply_rope(q.reshape(bsz, seq, N_HEADS, HEAD_DIM), cos_a, sin_a)
    k = apply_rope(k.reshape(bsz, seq, N_KV_HEADS, HEAD_DIM), cos_a, sin_a)
    v = v.reshape(bsz, seq, N_KV_HEADS, HEAD_DIM)
    q_idx = apply_rope(q_idx.reshape(bsz, seq, N_IDX_HEADS, IDX_DIM), cos_i, sin_i)
    k_idx = apply_rope(k_idx[:, :, None, :], cos_i, sin_i)[:, :, 0]
    y_attn = dsa_attention(q, k, v, q_idx, k_idx, w_idx)
    y_ssm = mamba_branch(xbc, z, dt_raw, conv_w, conv_b, dt_bias, a_log, d_skip, g_ssm_norm)
    gates = jax.nn.sigmoid(u @ w_gate).reshape(bsz, seq, 2, D_MODEL)
    merged = gates[:, :, 0] * (y_attn @ w_attn_branch) + gates[:, :, 1] * (y_ssm @ w_ssm_branch)
    return merged @ w_out


def setup_inputs(seed: int = 0) -> dict:
    key = jax.random.key(seed)
    ks = jax.random.split(key, 20)
    f32 = jnp.float32
    L = DEPTH

    def nrm(k, shape, fan_in):
        return jax.random.normal(k, shape, f32) * (fan_in ** -0.5)

    def gain(k, shape):
        return 1.0 + 0.02 * jax.random.normal(k, shape, f32)

    dt0 = jnp.exp(jax.random.uniform(ks[8], (L, SSM_HEADS), f32, minval=math.log(1e-3), maxval=math.log(1e-1)))
    return {
        'x': jax.random.normal(ks[0], (BATCH, SEQ, D_MODEL), f32),
        'w_in': nrm(ks[1], (L, D_MODEL, IN_COLS), D_MODEL),
        'w_gate': nrm(ks[2], (L, D_MODEL, 2 * D_MODEL), D_MODEL),
        'w_attn_branch': nrm(ks[3], (L, ATTN_WIDTH, D_MODEL), ATTN_WIDTH),
        'w_ssm_branch': nrm(ks[4], (L, SSM_WIDTH, D_MODEL), SSM_WIDTH),
        'w_out': nrm(ks[5], (L, D_MODEL, D_MODEL), D_MODEL),
        'conv_w': nrm(ks[6], (L, CONV_WIDTH, CONV_CH), CONV_WIDTH),
        'conv_b': 0.02 * jax.random.normal(ks[7], (L, CONV_CH), f32),
        'dt_bias': dt0 + jnp.log(-jnp.expm1(-dt0)),
        'a_log': jnp.log(jax.random.uniform(ks[9], (L, SSM_HEADS), f32, minval=1.0, maxval=16.0)),
        'd_skip': 1.0 + 0.1 * jax.random.normal(ks[10], (L, SSM_HEADS), f32),
        'g_ssm_norm': gain(ks[11], (L, SSM_WIDTH)),
        'g_mix': gain(ks[12], (L, D_MODEL)),
        'g_ffn': gain(ks[13], (L, D_MODEL)),
        'w_ffn_in': nrm(ks[14], (L, D_MODEL, 2 * D_FF), D_MODEL),
        'w_ffn_out': nrm(ks[15], (L, D_FF, D_MODEL), D_FF),
        'g_final': gain(ks[16], (D_MODEL,)),
    }


def reference(x, w_in, w_gate, w_attn_branch, w_ssm_branch, w_out, conv_w, conv_b, dt_bias, a_log, d_skip,
              g_ssm_norm, g_mix, g_ffn, w_ffn_in, w_ffn_out, g_final):
    h = x
    for i in range(DEPTH):
        u = rms_norm(h, g_mix[i])
        h = h + hybrid_mixer(u, w_in[i], w_gate[i], w_attn_branch[i], w_ssm_branch[i], w_out[i], conv_w[i],
                             conv_b[i], dt_bias[i], a_log[i], d_skip[i], g_ssm_norm[i])
        u2 = rms_norm(h, g_ffn[i])
        gate, up = jnp.split(u2 @ w_ffn_in[i], 2, axis=-1)
        h = h + (jax.nn.silu(gate) * up) @ w_ffn_out[i]
    return rms_norm(h, g_final)
```

```python
import os
import numpy as np
from contextlib import ExitStack
import concourse.bass as bass
import concourse.mybir as mybir
from concourse.bass_utils import run_bass_kernel_spmd

F32 = mybir.dt.float32
BF16 = mybir.dt.bfloat16
ALU = mybir.AluOpType
AF = mybir.ActivationFunctionType
AX = mybir.AxisListType
NEG = -1.0e30

REAL = dict(D=2048, S=16384, NH=16, NKV=4, NIH=16, SW=4096, G=8, DFF=5632, TGB=4)


def derive(C):
    C = dict(C)
    C["HD"] = 128; C["ID"] = 64; C["P"] = 64; C["N"] = 128; C["TOPK"] = min(256, C["S"] // 4)
    C["SH"] = C["SW"] // 64
    C["HG"] = C["SH"] // C["G"]
    C["CONVC"] = C["SW"] + 2 * C["G"] * 128
    sizes = (C["NH"] * 128, C["NKV"] * 128, C["NKV"] * 128, C["NIH"] * 64, 64, C["NIH"], C["SW"], C["CONVC"], C["SH"])
    offs = np.concatenate([[0], np.cumsum(sizes)]).astype(int)
    C["OFF"] = dict(zip(["q", "k", "v", "qi", "ki", "wi", "z", "xbc", "dt", "end"], [int(o) for o in offs]))
    C["INC"] = int(offs[-1])
    C["T"] = C["S"] // 4
    C["WIN"] = C["S"]
    C["KT"] = C["D"] // 128
    return C


class Buf:
    def __init__(self, t, name=""):
        self.t = t; self.name = name
        self.w = {}; self.r = {}
        self.dsem = None; self.dcnt = 0

    def __getitem__(self, k):
        return self.t[k]


class Eng:
    def __init__(self, K, eng, name, is_pe=False):
        self.eng = eng; self.name = name
        self.sem = K.new_sem("s_" + name)
        self.cnt = 0; self.seen = {}; self.is_pe = is_pe; self.old = []


class Kern:
    def __init__(self, nc, es):
        self.nc = nc; self.es = es; self.pes = es
        self.pe = Eng(self, nc.tensor, "pe", True)
        self.act = Eng(self, nc.scalar, "act")
        self.dve = Eng(self, nc.vector, "dve")
        self.pool = Eng(self, nc.gpsimd, "pool")
        self.sp = Eng(self, nc.sync, "sp")
        self.engs = [self.pe, self.act, self.dve, self.pool, self.sp]
        self.bufs = []
        self.uid = 0
        self.dold = []

    def new_sem(self, name):
        return self.es.enter_context(self.nc.semaphore(name))

    def sb(self, name, shape, dt):
        self.uid += 1
        b = Buf(self.pes.enter_context(self.nc.sbuf_tensor(f"{name}_{self.uid}", shape, dt)), name)
        self.bufs.append(b); return b

    def ps(self, name, shape, dt):
        self.uid += 1
        b = Buf(self.pes.enter_context(self.nc.psum_tensor(f"{name}_{self.uid}", shape, dt)), name)
        b.psum = True
        self.bufs.append(b); return b

    def dram(self, name, shape, dt, kind="Internal"):
        b = Buf(self.nc.dram_tensor(name, shape, dt, kind=kind), name)
        self.bufs.append(b); return b

    def _waits(self, e, reads, writes):
        need = {}
        for b in reads:
            for s, v in b.w.items():
                if need.get(s, 0) < v: need[s] = v
            if getattr(b, "psum", False):
                for s, v in b.r.items():
                    if s is not e.sem and need.get(s, 0) < v: need[s] = v
        for b in writes:
            for d in (b.w, b.r):
                for s, v in d.items():
                    if need.get(s, 0) < v: need[s] = v
        for s, v in need.items():
            if e.is_pe and (s is e.sem or any(s is so for so, _ in e.old)):
                continue
            if e.seen.get(s, 0) >= v:
                continue
            e.eng.wait_ge(s, v)
            e.seen[s] = v

    SEM_LIMIT = 30000

    def op(self, e, f, reads=(), writes=()):
        if e.cnt >= self.SEM_LIMIT:
            e.old.append((e.sem, e.cnt))
            self.uid += 1
            e.sem = self.new_sem(f"s_{e.name}_{self.uid}"); e.cnt = 0
        self._waits(e, reads, writes)
        ins = f()
        e.cnt += 1
        ins.then_inc(e.sem, 1)
        for b in reads:
            b.r[e.sem] = e.cnt
        for b in writes:
            b.w = {e.sem: e.cnt}; b.r = {}

    def dma(self, q, out_buf, out_ap, in_buf, in_ap, sem_buf=None):
        sb_ = sem_buf or out_buf
        if sb_.dsem is not None and sb_.dcnt >= self.SEM_LIMIT:
            self.dold.append((sb_.dsem, sb_.dcnt))
            sb_.dsem = None; sb_.dcnt = 0
        if sb_.dsem is None:
            self.uid += 1
            sb_.dsem = self.new_sem(f"d{self.uid}")
        self._waits(q, [in_buf], [out_buf])
        ins = q.eng.dma_start(out=out_ap, in_=in_ap)
        sb_.dcnt += 16
        ins.then_inc(sb_.dsem, 16)
        in_buf.r[sb_.dsem] = sb_.dcnt
        if set(out_buf.w.keys()) == {sb_.dsem} and not out_buf.r:
            out_buf.w = {sb_.dsem: sb_.dcnt}
        else:
            out_buf.w = {sb_.dsem: sb_.dcnt}; out_buf.r = {}

    def barrier(self):
        allneed = {}
        for e in self.engs:
            if e.cnt: allneed[e.sem] = e.cnt
            for (so, vo) in e.old[-1:]:
                allneed[so] = vo
        for (so, vo) in self.dold:
            allneed[so] = vo
        for b in self.bufs:
            if b.dsem is not None and b.dcnt:
                allneed[b.dsem] = b.dcnt
        for e in self.engs:
            for s, v in allneed.items():
                if s is e.sem: continue
                if e.seen.get(s, 0) >= v: continue
                e.eng.wait_ge(s, v); e.seen[s] = v
        for b in self.bufs:
            b.w = {}; b.r = {}

    def phase(self):
        K = self

        class _P:
            def __enter__(s):
                s.es = ExitStack(); s.es.__enter__(); K.pes = s.es; return s

            def __exit__(s, *a):
                K.barrier()
                K.barrier2()
                r = s.es.__exit__(*a); K.pes = K.es; return r
        return _P()

    def barrier2(self):
        nc = self.nc
        marks = {}
        for e in self.engs:
            if e is self.sp or e is self.pe:
                continue
        return


class _Skip:
    def __enter__(self):
        return None

    def __exit__(self, *a):
        return False


class Ring:
    def __init__(self, bufs):
        self.bufs = bufs; self.i = 0

    def next(self):
        b = self.bufs[self.i % len(self.bufs)]; self.i += 1; return b


def build(C, dbg=(), stop_after=None):
    C = derive(C)
    D, WIN, T, KT = C["D"], C["WIN"], C["T"], C["KT"]
    NH, NKV, NIH, SW, G, SH, HG, DFF, CONVC, INC = C["NH"], C["NKV"], C["NIH"], C["SW"], C["G"], C["SH"], C["HG"], C["DFF"], C["CONVC"], C["INC"]
    OFF = C["OFF"]; TGB = C["TGB"]; TG = TGB * 128
    OWN0 = WIN - T
    NBW = WIN // 128; NBO = T // 128; NGW = WIN // TG; NGO = T // TG
    GRP = NH // NKV
    nc = bass.Bass("TRN2", target_bir_lowering=False)
    order = ["P0", "P1", "P2", "P3", "P4"]
    enabled = set(order if stop_after is None else order[:order.index(stop_after) + 1])
    es = ExitStack()
    with es:
        es.enter_context(nc.allow_non_contiguous_dma(reason="small constant/param loads"))
        K = Kern(nc, es)
        pe, act, dve, pool, sp = K.pe, K.act, K.dve, K.pool, K.sp
        EI = "ExternalInput"
        xw = K.dram("xw", [WIN, D], F32, EI)
        valid = K.dram("valid", [WIN, 1], F32, EI)
        kbias = K.dram("kbias", [1, WIN], F32, EI)
        ca = K.dram("ca", [128, WIN], F32, EI); sa = K.dram("sa", [128, WIN], F32, EI)
        ci = K.dram("ci", [64, WIN], F32, EI); si = K.dram("si", [64, WIN], F32, EI)
        c_ident = K.dram("c_ident", [128, 128], F32, EI)
        c_pt128 = K.dram("c_pt128", [128, 128], F32, EI)
        c_pt64 = K.dram("c_pt64", [64, 64], F32, EI)
        c_dmask = K.dram("c_dmask", [128, 128], F32, EI)
        c_umat = K.dram("c_umat", [64, 64], F32, EI)
        c_lmat = K.dram("c_lmat", [64, 64], F32, EI)
        c_caus = K.dram("c_caus", [64, 64], F32, EI)
        w_in = K.dram("w_in", [D, INC], F32, EI)
        w_gate = K.dram("w_gate", [D, 2 * D], F32, EI)
        w_ab = K.dram("w_ab", [NH * 128, D], F32, EI)
        w_sb = K.dram("w_sb", [SW, D], F32, EI)
        w_out = K.dram("w_out", [D, D], F32, EI)
        conv_w = K.dram("conv_w", [4, CONVC], F32, EI)
        conv_b = K.dram("conv_b", [1, CONVC], F32, EI)
        dt_bias = K.dram("dt_bias", [1, SH], F32, EI)
        a_log = K.dram("a_log", [1, SH], F32, EI)
        d_skip = K.dram("d_skip", [1, SH], F32, EI)
        g_ssm = K.dram("g_ssm", [1, SW], F32, EI)
        g_mix = K.dram("g_mix", [1, D], F32, EI)
        g_ffn = K.dram("g_ffn", [1, D], F32, EI)
        w_f1 = K.dram("w_f1", [D, 2 * DFF], F32, EI)
        w_f2 = K.dram("w_f2", [DFF, D], F32, EI)
        g_fin = K.dram("g_fin", [1, D], F32, EI)
        out = K.dram("out", [T, D], F32, "ExternalOutput")
        W_in = K.dram("W_in", [D, INC], BF16); W_gate = K.dram("W_gate", [D, 2 * D], BF16)
        W_ab = K.dram("W_ab", [NH * 128, D], BF16); W_sb = K.dram("W_sb", [SW, D], BF16)
        W_out = K.dram("W_out", [D, D], BF16); W_f1 = K.dram("W_f1", [D, 2 * DFF], BF16)
        W_f2 = K.dram("W_f2", [DFF, D], BF16)
        dk = lambda n: ("ExternalOutput" if n in dbg else "Internal")
        KTs = K.dram("KTs", [NKV * 128, WIN], BF16, dk("KTs"))
        KITs = K.dram("KITs", [64, WIN], BF16, dk("KITs"))
        Vs = K.dram("Vs", [WIN, NKV * 128], BF16, dk("Vs"))
        XBCT = K.dram("XBCT", [NGW, CONVC, TG], BF16, dk("XBCT"))
        DTs = K.dram("DTs", [WIN, SH], F32, dk("DTs"))
        QTs = K.dram("QTs", [NH * 128, T], BF16, dk("QTs"))
        QITs = K.dram("QITs", [NIH * 64, T], BF16, dk("QITs"))
        WIDX = K.dram("WIDX", [T, NIH], F32, dk("WIDX"))
        Zs = K.dram("Zs", [T, SW], F32, dk("Zs"))
        GTs = K.dram("GTs", [T, 2 * D], F32, dk("GTs"))
        YAT = K.dram("YAT", [NH * 128, T], BF16, dk("YAT"))
        YST = K.dram("YST", [SW, T], BF16, dk("YST"))
        THR = K.dram("THR", [T, 1], F32, dk("THR")) if "THR" in dbg else None

        def dma3(q, out_b, out_ap, in_b, in_ap, sem_buf=None, maxk=16):
            nk = out_ap.shape[1]
            for k0 in range(0, nk, maxk):
                k1 = min(nk, k0 + maxk)
                K.dma(q, out_b, out_ap[:, k0:k1, :], in_b, in_ap[:, k0:k1, :], sem_buf=sem_buf)

        def mm(out_b, out_ap, pairs):
            n = len(pairs)
            for i, (lb, lap, rb, rap) in enumerate(pairs):
                K.op(pe, lambda: nc.tensor.matmul(out_ap, lhsT=lap, rhs=rap, start=(i == 0), stop=(i == n - 1)),
                     [lb, rb], [out_b])

        cast_engs = [act, dve, pool]

        def cast(i, out_b, out_ap, in_b, in_ap):
            e = cast_engs[i % 2] if getattr(in_b, "psum", False) else cast_engs[i % 3]
            if e is act:
                K.op(act, lambda: nc.scalar.copy(out=out_ap, in_=in_ap), [in_b], [out_b])
            elif e is dve:
                K.op(dve, lambda: nc.vector.tensor_copy(out=out_ap, in_=in_ap), [in_b], [out_b])
            else:
                K.op(pool, lambda: nc.gpsimd.tensor_copy(out=out_ap, in_=in_ap), [in_b], [out_b])

        with (K.phase() if "P0" in enabled else _Skip()) as _ph:
          if _ph is not None:
              CW = 2048
              r32 = Ring([K.sb("c32", [128, CW], F32) for _ in range(3)])
              r16 = Ring([K.sb("c16", [128, CW], BF16) for _ in range(3)])
              it = 0
              for (src, dst, rows, cols) in [(w_in, W_in, D, INC), (w_gate, W_gate, D, 2 * D), (w_ab, W_ab, NH * 128, D),
                                             (w_sb, W_sb, SW, D), (w_out, W_out, D, D), (w_f1, W_f1, D, 2 * DFF), (w_f2, W_f2, DFF, D)]:
                  for r0 in range(0, rows, 128):
                      for c0 in range(0, cols, CW):
                          cw = min(CW, cols - c0)
                          a = r32.next(); b = r16.next()
                          K.dma(sp, a, a[:, 0:cw], src, src[r0:r0 + 128, c0:c0 + cw])
                          cast(it, b, b[:, 0:cw], a, a[:, 0:cw]); it += 1
                          K.dma(pool, dst, dst[r0:r0 + 128, c0:c0 + cw], b, b[:, 0:cw], sem_buf=b)

        with (K.phase() if "P1" in enabled else _Skip()) as _ph:
          if _ph is not None:
              ident32 = K.sb("ident32", [128, 128], F32); ident = K.sb("ident", [128, 128], BF16)
              pt128_32 = K.sb("pt128_32", [128, 128], F32); pt128 = K.sb("pt128", [128, 128], BF16)
              pt64_32 = K.sb("pt64_32", [64, 64], F32); pt64 = K.sb("pt64", [64, 64], BF16)
              K.dma(sp, ident32, ident32[:], c_ident, c_ident[:]); cast(1, ident, ident[:], ident32, ident32[:])
              K.dma(sp, pt128_32, pt128_32[:], c_pt128, c_pt128[:]); cast(1, pt128, pt128[:], pt128_32, pt128_32[:])
              K.dma(sp, pt64_32, pt64_32[:], c_pt64, c_pt64[:]); cast(1, pt64, pt64[:], pt64_32, pt64_32[:])
              gmix = K.sb("gmix", [128, D], F32)
              K.dma(sp, gmix, gmix[:], g_mix, g_mix[0:1, :].partition_broadcast(128))
              NCT = CONVC // 128
              cwt = K.sb("cwt", [128, NCT, 4], F32); cbt = K.sb("cbt", [128, NCT], F32)
              cwr = K.sb("cwr", [NCT, 5, 128], F32)
              for k in range(4):
                  K.dma(sp, cwr, cwr[:, k, :], conv_w, conv_w[k, :].rearrange("(ct p) -> ct p", p=128))
              K.dma(sp, cwr, cwr[:, 4, :], conv_b, conv_b[0, :].rearrange("(ct p) -> ct p", p=128))
              pcw = K.ps("pcw", [128, 512], F32)
              for k in range(5):
                  K.op(pe, lambda: nc.tensor.transpose(out=pcw[:, k * NCT:(k + 1) * NCT], in_=cwr[:, k, :], identity=ident32[0:NCT, 0:NCT]), [cwr, ident32], [pcw])
              K.op(dve, lambda: nc.vector.tensor_copy(out=cwt[:], in_=pcw[:, 0:4 * NCT].rearrange("p (k c) -> p c k", c=NCT)), [pcw], [cwt])
              K.op(dve, lambda: nc.vector.tensor_copy(out=cbt[:], in_=pcw[:, 4 * NCT:5 * NCT]), [pcw], [cbt])
              dtb = K.sb("dtb", [128, SH], F32)
              K.dma(sp, dtb, dtb[:], dt_bias, dt_bias[0:1, :].partition_broadcast(128))
              halo = K.sb("halo", [128, NCT, 3], F32)
              K.op(dve, lambda: nc.vector.memset(halo[:], 0.0), [], [halo])
              xr = Ring([K.sb("xblk", [128, D], F32) for _ in range(2)])
              ub = Ring([K.sb("ublk", [128, D], BF16) for _ in range(2)])
              junk = K.sb("junk", [128, D], BF16)
              stat = Ring([K.sb("stat", [128, 4], F32) for _ in range(2)])
              uT = K.sb("uT", [128, KT, TG], BF16)
              wr = Ring([K.sb("wtile", [128, KT, 512], BF16) for _ in range(2)])
              psA = Ring([K.ps("psA", [128, 512], F32) for _ in range(3)])
              psR = Ring([K.ps("psR", [128, 512], F32) for _ in range(2)])
              psT = Ring([K.ps("psT", [128, 1024], BF16) for _ in range(2)])
              ev16 = Ring([K.sb("ev16", [128, 512], BF16) for _ in range(3)])
              ev32 = Ring([K.sb("ev32", [128, 512], F32) for _ in range(3)])
              ev32b = Ring([K.sb("ev32b", [128, 512], F32) for _ in range(2)])
              cbuf = Ring([K.sb("cbuf", [128, 3 + 512], F32) for _ in range(2)])
              cacc = Ring([K.sb("cacc", [128, 512], F32) for _ in range(2)])
              rc = Ring([K.sb("rc", [128, 512], F32) for _ in range(2)])
              rs = Ring([K.sb("rs", [128, 512], F32) for _ in range(2)])
              vl = Ring([K.sb("vl", [128, 1], F32) for _ in range(2)])

              def loadw(Wd, col0, n):
                  w = wr.next()
                  K.dma(sp, w, w[:, :, 0:n], Wd, Wd[:, col0:col0 + n].rearrange("(kt p) n -> p kt n", p=128))
                  return w

              def rope_store(ps, m, tok0, cos_d, sin_d, ptm, dst, row0, col0):
                  SUB = int(os.environ.get("P1SUB", "99"))
                  if SUB < 3: return
                  qb = ev16.next()
                  K.op(act, lambda: nc.scalar.copy(out=qb[0:m, 0:TG], in_=ps[0:m, 0:TG]), [ps], [qb])
                  if SUB < 4: return
                  p2 = psR.next()
                  mm(p2, p2[0:m, 0:TG], [(ptm, ptm[0:m, 0:m], qb, qb[0:m, 0:TG])])
                  if SUB < 5: return
                  cc = rc.next(); ss = rs.next()
                  K.dma(sp, cc, cc[0:m, 0:TG], cos_d, cos_d[0:m, tok0:tok0 + TG])
                  K.dma(sp, ss, ss[0:m, 0:TG], sin_d, sin_d[0:m, tok0:tok0 + TG])
                  t1 = ev32.next(); t2 = ev32b.next()
                  K.op(dve, lambda: nc.vector.tensor_tensor(out=t1[0:m, 0:TG], in0=ps[0:m, 0:TG], in1=cc[0:m, 0:TG], op=ALU.mult), [ps, cc], [t1])
                  K.op(dve, lambda: nc.vector.tensor_tensor(out=t2[0:m, 0:TG], in0=p2[0:m, 0:TG], in1=ss[0:m, 0:TG], op=ALU.mult), [p2, ss], [t2])
                  if SUB < 6: return
                  ob = ev16.next()
                  if os.environ.get("P1POOL", "1") == "1":
                      K.op(pool, lambda: nc.gpsimd.tensor_tensor(out=ob[0:m, 0:TG], in0=t1[0:m, 0:TG], in1=t2[0:m, 0:TG], op=ALU.add), [t1, t2], [ob])
                  else:
                      K.op(dve, lambda: nc.vector.tensor_tensor(out=ob[0:m, 0:TG], in0=t1[0:m, 0:TG], in1=t2[0:m, 0:TG], op=ALU.add), [t1, t2], [ob])
                  K.dma(pool, dst, dst[row0:row0 + m, col0:col0 + TG], ob, ob[0:m, 0:TG], sem_buf=ob)

              def projB(Wd, col0, m):
                  w = loadw(Wd, col0, m)
                  ps = psA.next()
                  if int(os.environ.get("P1SUB", "99")) < 2: return ps
                  mm(ps, ps[0:m, 0:TG], [(w, w[:, k, 0:m], uT, uT[:, k, :]) for k in range(KT)])
                  return ps

              def projA(Wd, col0, n, blk):
                  pass

              LVL = int(os.environ.get("P1LVL", "99"))
              for g in range(NGW if LVL >= 2 else 0):
                  tok0 = g * TG
                  own = tok0 >= OWN0
                  for b in range(TGB):
                      xb = xr.next(); u = ub.next(); st = stat.next()
                      K.dma(sp, xb, xb[:], xw, xw[tok0 + b * 128: tok0 + (b + 1) * 128, :])
                      K.op(act, lambda: nc.scalar.activation(out=junk[:], in_=xb[:], func=AF.Square, accum_out=st[:, 0:1]), [xb], [junk, st])
                      K.op(act, lambda: nc.scalar.activation(out=st[:, 1:2], in_=st[:, 0:1], func=AF.Sqrt, scale=1.0 / D, bias=1e-6), [st], [st])
                      K.op(dve, lambda: nc.vector.reciprocal(out=st[:, 2:3], in_=st[:, 1:2]), [st], [st])
                      K.op(dve, lambda: nc.vector.scalar_tensor_tensor(out=u[:], in0=xb[:], scalar=st[:, 2:3], in1=gmix[:], op0=ALU.mult, op1=ALU.mult), [xb, st, gmix], [u])
                      for k0 in range(0, KT, 8):
                          kn = min(8, KT - k0)
                          pt = psT.next()
                          for kk in range(kn):
                              K.op(pe, lambda: nc.tensor.transpose(out=pt[:, kk * 128:(kk + 1) * 128], in_=u[:, (k0 + kk) * 128:(k0 + kk + 1) * 128], identity=ident[:]), [u, ident], [pt])
                          cast(k0 // 8, uT, uT[:, k0:k0 + kn, b * 128:(b + 1) * 128], pt, pt[:, 0:kn * 128].rearrange("p (k t) -> p k t", t=128))
                  for h in range(NKV if LVL >= 3 else 0):
                      ps = projB(W_in, OFF["k"] + h * 128, 128)
                      rope_store(ps, 128, tok0, ca, sa, pt128, KTs, h * 128, tok0)
                  if LVL >= 3 and os.environ.get("P1KI", "1") == "1":
                      ps = projB(W_in, OFF["ki"], 64)
                      rope_store(ps, 64, tok0, ci, si, pt64, KITs, 0, tok0)
                  for ct in range(NCT if LVL >= 4 else 0):
                      ps = projB(W_in, OFF["xbc"] + ct * 128, 128)
                      cb_ = cbuf.next(); ac = cacc.next()
                      K.op(act, lambda: nc.scalar.copy(out=cb_[:, 0:3], in_=halo[:, ct, :]), [halo], [cb_])
                      K.op(act, lambda: nc.scalar.copy(out=cb_[:, 3:3 + TG], in_=ps[:, 0:TG]), [ps], [cb_])
                      K.op(pool, lambda: nc.gpsimd.tensor_copy(out=halo[:, ct, :], in_=cb_[:, TG:TG + 3]), [cb_], [halo])
                      K.op(dve, lambda: nc.vector.tensor_scalar(out=ac[:, 0:TG], in0=cb_[:, 3:3 + TG], scalar1=cwt[:, ct, 3:4], scalar2=None, op0=ALU.mult), [cb_, cwt], [ac])
                      for k in range(3):
                          K.op(dve, lambda: nc.vector.scalar_tensor_tensor(out=ac[:, 0:TG], in0=cb_[:, k:k + TG], scalar=cwt[:, ct, k:k + 1], in1=ac[:, 0:TG], op0=ALU.mult, op1=ALU.add), [cb_, cwt, ac], [ac])
                      ob = ev16.next()
                      K.op(act, lambda: nc.scalar.activation(out=ob[:, 0:TG], in_=ac[:, 0:TG], func=AF.Silu, bias=cbt[:, ct:ct + 1], scale=1.0), [ac, cbt], [ob])
                      K.dma(pool, XBCT, XBCT[g, ct * 128:(ct + 1) * 128, :], ob, ob[:, 0:TG], sem_buf=ob)
                  segs = [("v", W_in, OFF["v"], NKV * 128), ("dt", W_in, OFF["dt"], SH)]
                  if own:
                      segs += [("wi", W_in, OFF["wi"], NIH), ("z", W_in, OFF["z"], SW), ("g", W_gate, 0, 2 * D)]
                  for (nm, Wd, c0, ncols) in (segs if LVL >= 5 else []):
                      for n0 in range(0, ncols, 512):
                          n = min(512, ncols - n0)
                          w = loadw(Wd, c0 + n0, n)
                          for b in range(TGB):
                              t0 = tok0 + b * 128
                              ps = psA.next()
                              mm(ps, ps[:, 0:n], [(uT, uT[:, k, b * 128:(b + 1) * 128], w, w[:, k, 0:n]) for k in range(KT)])
                              if nm == "v":
                                  ob = ev16.next()
                                  K.op(act, lambda: nc.scalar.copy(out=ob[:, 0:n], in_=ps[:, 0:n]), [ps], [ob])
                                  K.dma(pool, Vs, Vs[t0:t0 + 128, n0:n0 + n], ob, ob[:, 0:n], sem_buf=ob)
                              elif nm == "dt":
                                  o1 = ev32.next(); v_ = vl.next()
                                  K.dma(sp, v_, v_[:], valid, valid[t0:t0 + 128, :])
                                  K.op(dve, lambda: nc.vector.tensor_tensor(out=o1[:, 0:n], in0=ps[:, 0:n], in1=dtb[:, 0:n], op=ALU.add), [ps, dtb], [o1])
                                  K.op(act, lambda: nc.scalar.activation(out=o1[:, 0:n], in_=o1[:, 0:n], func=AF.Exp), [o1], [o1])
                                  K.op(act, lambda: nc.scalar.activation(out=o1[:, 0:n], in_=o1[:, 0:n], func=AF.Ln, bias=1.0, scale=1.0), [o1], [o1])
                                  K.op(dve, lambda: nc.vector.tensor_scalar(out=o1[:, 0:n], in0=o1[:, 0:n], scalar1=v_[:, 0:1], scalar2=None, op0=ALU.mult), [o1, v_], [o1])
                                  K.dma(pool, DTs, DTs[t0:t0 + 128, 0:n], o1, o1[:, 0:n], sem_buf=o1)
                              else:
                                  o1 = ev32.next()
                                  to = t0 - OWN0
                                  if nm == "g":
                                      K.op(act, lambda: nc.scalar.activation(out=o1[:, 0:n], in_=ps[:, 0:n], func=AF.Sigmoid), [ps], [o1])
                                      dst = GTs
                                  else:
                                      K.op(act, lambda: nc.scalar.copy(out=o1[:, 0:n], in_=ps[:, 0:n]), [ps], [o1])
                                      dst = WIDX if nm == "wi" else Zs
                                  K.dma(pool, dst, dst[to:to + 128, n0:n0 + n], o1, o1[:, 0:n], sem_buf=o1)
                  if own and LVL >= 6:
                      to0 = tok0 - OWN0
                      for h in range(NH):
                          ps = projB(W_in, OFF["q"] + h * 128, 128)
                          rope_store(ps, 128, tok0, ca, sa, pt128, QTs, h * 128, to0)
                      for h in range(NIH):
                          ps = projB(W_in, OFF["qi"] + h * 64, 64)
                          rope_store(ps, 64, tok0, ci, si, pt64, QITs, h * 64, to0)

        with (K.phase() if "P2" in enabled else _Skip()) as _ph:
          if _ph is not None:
              ident32 = K.sb("ident32", [128, 128], F32); ident = K.sb("ident", [128, 128], BF16)
              K.dma(sp, ident32, ident32[:], c_ident, c_ident[:]); cast(1, ident, ident[:], ident32, ident32[:])
              dm = K.sb("dm", [128, 128], F32)
              K.dma(sp, dm, dm[:], c_dmask, c_dmask[:])
              ones = K.sb("ones", [128, 128], BF16)
              K.op(dve, lambda: nc.vector.memset(ones[:], 1.0), [], [ones])
              IS = K.sb("IS", [128, WIN], F32)
              Mb = K.sb("Mb", [128, WIN], BF16)
              MT = K.sb("MT", [128, NBW, 128], BF16)
              qt = Ring([K.sb("qt", [128, NH, 128], BF16) for _ in range(2)])
              qit = Ring([K.sb("qit", [64, NIH, 128], BF16) for _ in range(2)])
              wix = Ring([K.sb("wix", [128, NIH], F32) for _ in range(2)])
              kb32 = Ring([K.sb("kb32", [1, 512], F32) for _ in range(2)])
              kb16 = Ring([K.sb("kb16", [1, 512], BF16) for _ in range(2)])
              kit = Ring([K.sb("kit", [64, 512], BF16) for _ in range(2)])
              rl = Ring([K.sb("rl", [128, 512], F32) for _ in range(3)])
              psI = Ring([K.ps("psI", [128, 512], F32) for _ in range(3)])
              psT = Ring([K.ps("psT2", [128, 1024], BF16) for _ in range(1)])
              psS = Ring([K.ps("psS", [128, 512], F32) for _ in range(2)])
              psO = K.ps("psO", [128, 512], F32); psM = K.ps("psM", [128, 512], F32)
              st = K.sb("bst", [128, 8], F32)
              KCH = 2048
              ktl = Ring([K.sb("ktl", [128, KCH], BF16) for _ in range(2)])
              vtl = Ring([K.sb("vtl", [128, KCH // 128, 128], BF16) for _ in range(2)])
              eb = Ring([K.sb("eb", [128, GRP, 128], BF16) for _ in range(3)])
              pb = Ring([K.sb("pb", [128, GRP, 128], BF16) for _ in range(3)])
              rcp = K.sb("rcp", [128, 512], F32)
              yo = Ring([K.sb("yo", [128, GRP, 128], BF16) for _ in range(2)])
              NIT = 22
              scale = 1.0 / float(np.sqrt(128.0))
              for j in range(NBO):
                  LIM = OWN0 + 128 * (j + 1); NKTL = LIM // 128
                  q = qt.next(); qi = qit.next(); wi = wix.next()
                  dma3(sp, q, q[:], QTs, QTs[:, j * 128:(j + 1) * 128].rearrange("(h d) t -> d h t", d=128), maxk=8)
                  K.dma(sp, qi, qi[:], QITs, QITs[:, j * 128:(j + 1) * 128].rearrange("(h d) t -> d h t", d=64))
                  K.dma(sp, wi, wi[:], WIDX, WIDX[j * 128:(j + 1) * 128, :])
                  for c0 in range(0, LIM, 512):
                      n = min(512, LIM - c0)
                      a = kb32.next(); b16 = kb16.next(); kt_ = kit.next()
                      K.dma(sp, a, a[:, 0:n], kbias, kbias[:, c0:c0 + n])
                      K.op(act, lambda: nc.scalar.copy(out=b16[:, 0:n], in_=a[:, 0:n]), [a], [b16])
                      K.dma(sp, kt_, kt_[:, 0:n], KITs, KITs[:, c0:c0 + n])
                      ps = psI.next()
                      mm(ps, ps[:, 0:n], [(ones, ones[0:1, :], b16, b16[0:1, 0:n])])
                      K.op(act, lambda: nc.scalar.copy(out=IS[:, c0:c0 + n], in_=ps[:, 0:n]), [ps], [IS])
                      for h in range(NIH):
                          ps = psI.next(); r_ = rl.next()
                          mm(ps, ps[:, 0:n], [(qi, qi[:, h, :], kt_, kt_[:, 0:n])])
                          K.op(act, lambda: nc.scalar.activation(out=r_[:, 0:n], in_=ps[:, 0:n], func=AF.Relu), [ps], [r_])
                          K.op(dve, lambda: nc.vector.scalar_tensor_tensor(out=IS[:, c0:c0 + n], in0=r_[:, 0:n], scalar=wi[:, h:h + 1], in1=IS[:, c0:c0 + n], op0=ALU.mult, op1=ALU.add), [r_, wi, IS], [IS])
                  K.op(dve, lambda: nc.vector.tensor_tensor(out=IS[:, LIM - 128:LIM], in0=IS[:, LIM - 128:LIM], in1=dm[:], op=ALU.add), [IS, dm], [IS])
                  K.op(dve, lambda: nc.vector.reduce_max(out=st[:, 0:1], in_=IS[:, 0:LIM], axis=AX.X), [IS], [st])
                  K.op(dve, lambda: nc.vector.tensor_scalar_add(out=st[:, 1:2], in0=st[:, 0:1], scalar1=-256.0), [st], [st])
                  hw = 256.0
                  for i in range(NIT):
                      K.op(dve, lambda: nc.vector.tensor_scalar(out=Mb[:, 0:LIM], in0=IS[:, 0:LIM], scalar1=st[:, 1:2], scalar2=None, op0=ALU.is_gt, op1=ALU.add, accum_out=st[:, 2:3]), [IS, st], [Mb, st])
                      K.op(dve, lambda: nc.vector.tensor_scalar(out=st[:, 3:4], in0=st[:, 2:3], scalar1=float(C["TOPK"]), scalar2=0.5, op0=ALU.is_ge, op1=ALU.subtract), [st], [st])
                      K.op(dve, lambda: nc.vector.scalar_tensor_tensor(out=st[:, 1:2], in0=st[:, 3:4], scalar=hw, in1=st[:, 1:2], op0=ALU.mult, op1=ALU.add), [st], [st])
                      hw = hw / 2.0
                  K.op(dve, lambda: nc.vector.tensor_scalar_add(out=st[:, 4:5], in0=st[:, 1:2], scalar1=-hw), [st], [st])
                  K.op(dve, lambda: nc.vector.tensor_scalar(out=Mb[:, 0:LIM], in0=IS[:, 0:LIM], scalar1=st[:, 4:5], scalar2=None, op0=ALU.is_gt), [IS, st], [Mb])
                  if THR is not None:
                      K.dma(sp, THR, THR[j * 128:(j + 1) * 128, :], st, st[:, 4:5], sem_buf=st)
                  for k0 in range(0, NKTL, 8):
                      kn = min(8, NKTL - k0)
                      pt = psT.next()
                      for kk in range(kn):
                          K.op(pe, lambda: nc.tensor.transpose(out=pt[:, kk * 128:(kk + 1) * 128], in_=Mb[:, (k0 + kk) * 128:(k0 + kk + 1) * 128], identity=ident[:]), [Mb, ident], [pt])
                      cast(k0 // 8, MT, MT[:, k0:k0 + kn, :], pt, pt[:, 0:kn * 128].rearrange("p (k t) -> p k t", t=128))
                  for kv in range(NKV):
                      for kt in range(NKTL):
                          if kt % (KCH // 128) == 0:
                              c0 = kt * 128; n = min(KCH, LIM - c0)
                              ktile = ktl.next(); vtile = vtl.next()
                              K.dma(sp, ktile, ktile[:, 0:n], KTs, KTs[kv * 128:(kv + 1) * 128, c0:c0 + n])
                              dma3(sp, vtile, vtile[:, 0:n // 128, :], Vs, Vs[c0:c0 + n, kv * 128:(kv + 1) * 128].rearrange("(k p) d -> p k d", p=128), maxk=8)
                          kk = kt % (KCH // 128)
                          ps = psS.next(); e_ = eb.next(); p_ = pb.next()
                          mm(ps, ps[:, 0:GRP * 128], [(ktile, ktile[:, kk * 128:(kk + 1) * 128], q, q[:, kv * GRP:(kv + 1) * GRP, :])])
                          K.op(act, lambda: nc.scalar.activation(out=e_[:], in_=ps[:, 0:GRP * 128].rearrange("p (g t) -> p g t", t=128), func=AF.Exp, scale=scale), [ps], [e_])
                          K.op(pool, lambda: nc.gpsimd.tensor_tensor(out=p_[:], in0=e_[:], in1=MT[:, kt, :].unsqueeze(1).to_broadcast([128, GRP, 128]), op=ALU.mult), [e_, MT], [p_])
                          K.op(pe, lambda: nc.tensor.matmul(psO[:, 0:GRP * 128], lhsT=vtile[:, kk, :], rhs=p_[:], start=(kt == 0), stop=(kt == NKTL - 1)), [vtile, p_], [psO])
                          K.op(pe, lambda: nc.tensor.matmul(psM[:, 0:GRP * 128], lhsT=ones[:], rhs=p_[:], start=(kt == 0), stop=(kt == NKTL - 1)), [ones, p_], [psM])
                      y_ = yo.next()
                      K.op(dve, lambda: nc.vector.reciprocal(out=rcp[:, 0:GRP * 128], in_=psM[:, 0:GRP * 128]), [psM], [rcp])
                      K.op(dve, lambda: nc.vector.tensor_tensor(out=y_[:], in0=psO[:, 0:GRP * 128].rearrange("p (g t) -> p g t", t=128), in1=rcp[:, 0:GRP * 128].rearrange("p (g t) -> p g t", t=128), op=ALU.mult), [psO, rcp], [y_])
                      K.dma(pool, YAT, YAT[kv * GRP * 128:(kv + 1) * GRP * 128, j * 128:(j + 1) * 128].rearrange("(g d) t -> d g t", d=128), y_, y_[:], sem_buf=y_)

        with (K.phase() if "P3" in enabled else _Skip()) as _ph:
          if _ph is not None:
              NXT = SW // 128; HP = HG * 64
              ident32 = K.sb("ident32", [128, 128], F32); ident = K.sb("ident", [128, 128], BF16)
              K.dma(sp, ident32, ident32[:], c_ident, c_ident[:]); cast(1, ident, ident[:], ident32, ident32[:])
              umat = K.sb("umat", [64, 64], F32); lmat = K.sb("lmat", [64, 64], F32); caus = K.sb("caus", [64, 64], F32)
              K.dma(sp, umat, umat[:], c_umat, c_umat[:]); K.dma(sp, lmat, lmat[:], c_lmat, c_lmat[:]); K.dma(sp, caus, caus[:], c_caus, c_caus[:])
              ones64 = K.sb("ones64", [64, 128], F32)
              K.op(dve, lambda: nc.vector.memset(ones64[:], 1.0), [], [ones64])
              ah = K.sb("ah", [64, SH], F32)
              K.dma(sp, ah, ah[:], a_log, a_log[0:1, :].partition_broadcast(64))
              K.op(act, lambda: nc.scalar.activation(out=ah[:], in_=ah[:], func=AF.Exp), [ah], [ah])
              K.op(dve, lambda: nc.vector.tensor_scalar(out=ah[:], in0=ah[:], scalar1=-1.0, scalar2=None, op0=ALU.mult), [ah], [ah])
              dsk = K.sb("dsk", [64, SH], F32)
              K.dma(sp, dsk, dsk[:], d_skip, d_skip[0:1, :].partition_broadcast(64))
              gsb = K.sb("gsb", [64, SW], F32)
              K.dma(sp, gsb, gsb[:], g_ssm, g_ssm[0:1, :].partition_broadcast(64))
              hT = K.sb("hT", [128, SH, 64], F32); hT16 = K.sb("hT16", [128, SH, 64], BF16)
              K.op(dve, lambda: nc.vector.memset(hT[:], 0.0), [], [hT])
              K.op(pool, lambda: nc.gpsimd.memset(hT16[:], 0.0), [], [hT16])
              xbt_r = Ring([K.sb("xbt", [128, NCT, 64], BF16) for _ in range(2)])
              dtc_r = Ring([K.sb("dtc", [64, SH], F32) for _ in range(2)])
              a32_r = Ring([K.sb("a32", [64, SH], F32) for _ in range(2)])
              acs_r = Ring([K.sb("acs", [64, SH], F32) for _ in range(2)])
              dtot_r = Ring([K.sb("dtot", [128, SH], F32) for _ in range(2)])
              tail_r = Ring([K.sb("tail", [64, SH], F32) for _ in range(2)])
              ee_r = Ring([K.sb("ee", [64, SH], F32) for _ in range(2)])
              w2_r = Ring([K.sb("w2", [64, SH], F32) for _ in range(2)])
              xtm_r = Ring([K.sb("xtm", [64, SH, 64], BF16) for _ in range(2)])
              btm_r = Ring([K.sb("btm", [64, G * 128], BF16) for _ in range(2)])
              xdt_r = Ring([K.sb("xdt", [64, SH, 64], BF16) for _ in range(1)])
              xdtt_r = Ring([K.sb("xdtt", [64, SH, 64], BF16) for _ in range(1)])
              rr_r = Ring([K.sb("rr", [64, HG, 64], F32) for _ in range(2)])
              dec_r = Ring([K.sb("dec", [64, HG, 64], F32) for _ in range(2)])
              cbm_r = Ring([K.sb("cbm", [64, 64], F32) for _ in range(2)])
              mt_r = Ring([K.sb("mt", [64, HG, 64], BF16) for _ in range(2)])
              tmpf_r = Ring([K.sb("tmpf", [64, HG, 64], F32) for _ in range(2)])
              ytm_r = Ring([K.sb("ytm", [64, SH, 64], F32) for _ in range(1)])
              zc_r = Ring([K.sb("zc", [64, SW], F32) for _ in range(1)])
              t2_r = Ring([K.sb("t2", [64, SH, 64], F32) for _ in range(1)])
              sq_j = K.sb("sqj", [64, SW // G], F32)
              gst_r = Ring([K.sb("gst", [64, 3 * G], F32) for _ in range(2)])
              yb_r = Ring([K.sb("yb", [64, SW], BF16) for _ in range(2)])
              yst_r = Ring([K.sb("yst", [128, NXT, 64], BF16) for _ in range(2)])
              psa = Ring([K.ps("psa", [128, 512], F32) for _ in range(2)])
              psd = Ring([K.ps("psd", [128, 512], F32) for _ in range(1)])
              psc = Ring([K.ps("psc", [128, 512], F32) for _ in range(1)])
              psy = Ring([K.ps("psy", [128, 512], F32) for _ in range(1)])
              psf = Ring([K.ps("psf", [128, 512], F32) for _ in range(1)])
              psh = Ring([K.ps("psh", [128, 512], F32) for _ in range(1)])
              pst = Ring([K.ps("pst", [128, 1024], BF16) for _ in range(1)])
              for ck in range(WIN // 64):
                  t0 = ck * 64
                  own = t0 >= OWN0
                  xbt = xbt_r.next(); dtc = dtc_r.next(); a32 = a32_r.next(); acs = acs_r.next(); dtot = dtot_r.next()
                  tail = tail_r.next(); w2 = w2_r.next(); xtm = xtm_r.next(); btm = btm_r.next(); xdtt = xdtt_r.next()
                  dma3(sp, xbt, xbt[:], XBCT, XBCT[t0 // TG, :, (t0 % TG):(t0 % TG) + 64].rearrange("(ct p) t -> p ct t", p=128))
                  K.dma(sp, dtc, dtc[:], DTs, DTs[t0:t0 + 64, :])
                  K.op(dve, lambda: nc.vector.tensor_tensor(out=a32[:], in0=dtc[:], in1=ah[:], op=ALU.mult), [dtc, ah], [a32])
                  pA = psa.next(); pB = psa.next()
                  mm(pA, pA[0:64, 0:SH], [(umat, umat[:], a32, a32[:])])
                  mm(pB, pB[:, 0:SH], [(ones64, ones64[:], a32, a32[:])])
                  K.op(act, lambda: nc.scalar.copy(out=acs[:], in_=pA[0:64, 0:SH]), [pA], [acs])
                  K.op(act, lambda: nc.scalar.activation(out=dtot[:], in_=pB[:, 0:SH], func=AF.Exp), [pB], [dtot])
                  K.op(dve, lambda: nc.vector.tensor_tensor(out=tail[:], in0=pB[0:64, 0:SH], in1=acs[:], op=ALU.subtract), [pB, acs], [tail])
                  K.op(act, lambda: nc.scalar.activation(out=tail[:], in_=tail[:], func=AF.Exp), [tail], [tail])
                  K.op(dve, lambda: nc.vector.tensor_tensor(out=w2[:], in0=tail[:], in1=dtc[:], op=ALU.mult), [tail, dtc], [w2])
                  for c0 in range(0, NXT + G, 8):
                      cn = min(8, NXT + G - c0)
                      pt = pst.next()
                      for cc in range(cn):
                          K.op(pe, lambda: nc.tensor.transpose(out=pt[0:64, cc * 128:(cc + 1) * 128], in_=xbt[:, c0 + cc, :], identity=ident[:]), [xbt, ident], [pt])
                      for cc in range(cn):
                          ct = c0 + cc
                          if ct < NXT:
                              cast(ct, xtm, xtm[:, ct * 2:(ct + 1) * 2, :], pt, pt[0:64, cc * 128:(cc + 1) * 128].rearrange("t (h p) -> t h p", p=64))
                          else:
                              cast(ct, btm, btm[:, (ct - NXT) * 128:(ct - NXT + 1) * 128], pt, pt[0:64, cc * 128:(cc + 1) * 128])
                  K.op(pool, lambda: nc.gpsimd.tensor_tensor(out=xdtt[:], in0=xtm[:], in1=w2[:].unsqueeze(2).to_broadcast([64, SH, 64]), op=ALU.mult), [xtm, w2], [xdtt])
                  if own:
                      to = t0 - OWN0
                      ee = ee_r.next(); xdt = xdt_r.next(); ytm = ytm_r.next()
                      K.op(act, lambda: nc.scalar.activation(out=ee[:], in_=acs[:], func=AF.Exp), [acs], [ee])
                      K.op(dve, lambda: nc.vector.tensor_tensor(out=xdt[:], in0=xtm[:], in1=dtc[:].unsqueeze(2).to_broadcast([64, SH, 64]), op=ALU.mult), [xtm, dtc], [xdt])
                      for g in range(G):
                          rr = rr_r.next(); dec = dec_r.next(); cbm = cbm_r.next(); mt = mt_r.next(); tmpf = tmpf_r.next()
                          K.op(dve, lambda: nc.vector.tensor_tensor(out=rr[:], in0=umat[:].unsqueeze(1).to_broadcast([64, HG, 64]), in1=a32[:, g * HG:(g + 1) * HG].unsqueeze(2).to_broadcast([64, HG, 64]), op=ALU.mult), [umat, a32], [rr])
                          pD = psd.next()
                          mm(pD, pD[0:64, 0:HP], [(lmat, lmat[:], rr, rr[:])])
                          K.op(act, lambda: nc.scalar.activation(out=dec[:], in_=pD[0:64, 0:HP].rearrange("s (h t) -> s h t", t=64), func=AF.Exp), [pD], [dec])
                          pC = psc.next()
                          mm(pC, pC[0:64, 0:64], [(xbt, xbt[:, NXT + g, :], xbt, xbt[:, NXT + G + g, :])])
                          K.op(dve, lambda: nc.vector.tensor_tensor(out=cbm[:], in0=pC[0:64, 0:64], in1=caus[:], op=ALU.mult), [pC, caus], [cbm])
                          K.op(dve, lambda: nc.vector.tensor_tensor(out=mt[:], in0=dec[:], in1=cbm[:].unsqueeze(1).to_broadcast([64, HG, 64]), op=ALU.mult), [dec, cbm], [mt])
                          pY = psy.next()
                          for h in range(HG):
                              K.op(pe, lambda: nc.tensor.matmul(pY[0:64, h * 64:(h + 1) * 64], lhsT=mt[:, h, :], rhs=xdt[:, g * HG + h, :], start=True, stop=True), [mt, xdt], [pY])
                          pF = psf.next()
                          mm(pF, pF[0:64, 0:HP], [(xbt, xbt[:, NXT + G + g, :], hT16, hT16[:, g * HG:(g + 1) * HG, :])])
                          K.op(dve, lambda: nc.vector.tensor_tensor(out=tmpf[:], in0=pF[0:64, 0:HP].rearrange("t (h p) -> t h p", p=64), in1=ee[:, g * HG:(g + 1) * HG].unsqueeze(2).to_broadcast([64, HG, 64]), op=ALU.mult), [pF, ee], [tmpf])
                          K.op(dve, lambda: nc.vector.tensor_tensor(out=ytm[:, g * HG:(g + 1) * HG, :], in0=pY[0:64, 0:HP].rearrange("t (h p) -> t h p", p=64), in1=tmpf[:], op=ALU.add), [pY, tmpf], [ytm])
                  for g in range(G):
                      pH = psh.next()
                      mm(pH, pH[:, 0:HP], [(btm, btm[:, g * 128:(g + 1) * 128], xdtt, xdtt[:, g * HG:(g + 1) * HG, :])])
                      K.op(dve, lambda: nc.vector.tensor_tensor(out=hT[:, g * HG:(g + 1) * HG, :], in0=hT[:, g * HG:(g + 1) * HG, :], in1=dtot[:, g * HG:(g + 1) * HG].unsqueeze(2).to_broadcast([128, HG, 64]), op=ALU.mult), [hT, dtot], [hT])
                      K.op(dve, lambda: nc.vector.tensor_tensor(out=hT[:, g * HG:(g + 1) * HG, :], in0=pH[:, 0:HP].rearrange("n (h p) -> n h p", p=64), in1=hT[:, g * HG:(g + 1) * HG, :], op=ALU.add), [pH, hT], [hT])
                      K.op(act, lambda: nc.scalar.copy(out=hT16[:, g * HG:(g + 1) * HG, :], in_=hT[:, g * HG:(g + 1) * HG, :]), [hT], [hT16])
                  if own:
                      zc = zc_r.next(); t2 = t2_r.next(); gst = gst_r.next(); yb = yb_r.next(); yst = yst_r.next()
                      K.dma(sp, zc, zc[:], Zs, Zs[to:to + 64, :])
                      K.op(pool, lambda: nc.gpsimd.tensor_tensor(out=t2[:], in0=xtm[:], in1=dsk[:].unsqueeze(2).to_broadcast([64, SH, 64]), op=ALU.mult), [xtm, dsk], [t2])
                      K.op(dve, lambda: nc.vector.tensor_tensor(out=ytm[:], in0=ytm[:], in1=t2[:], op=ALU.add), [ytm, t2], [ytm])
                      K.op(act, lambda: nc.scalar.activation(out=zc[:], in_=zc[:], func=AF.Silu), [zc], [zc])
                      yf = ytm[:].rearrange("t h p -> t (h p)")
                      K.op(dve, lambda: nc.vector.tensor_tensor(out=yf, in0=yf, in1=zc[:], op=ALU.mult), [ytm, zc], [ytm])
                      GW = SW // G
                      for g in range(G):
                          K.op(act, lambda: nc.scalar.activation(out=sq_j[:], in_=yf[:, g * GW:(g + 1) * GW], func=AF.Square, accum_out=gst[:, g:g + 1]), [ytm], [sq_j, gst])
                      K.op(act, lambda: nc.scalar.activation(out=gst[:, G:2 * G], in_=gst[:, 0:G], func=AF.Sqrt, scale=1.0 / GW, bias=1e-6), [gst], [gst])
                      K.op(dve, lambda: nc.vector.reciprocal(out=gst[:, 2 * G:3 * G], in_=gst[:, G:2 * G]), [gst], [gst])
                      for g in range(G):
                          K.op(dve, lambda: nc.vector.scalar_tensor_tensor(out=yb[:, g * GW:(g + 1) * GW], in0=yf[:, g * GW:(g + 1) * GW], scalar=gst[:, 2 * G + g:2 * G + g + 1], in1=gsb[:, g * GW:(g + 1) * GW], op0=ALU.mult, op1=ALU.mult), [ytm, gst, gsb], [yb])
                      for c0 in range(0, NXT, 16):
                          cn = min(16, NXT - c0)
                          pt = pst.next()
                          for cc in range(cn):
                              K.op(pe, lambda: nc.tensor.transpose(out=pt[:, cc * 64:(cc + 1) * 64], in_=yb[:, (c0 + cc) * 128:(c0 + cc + 1) * 128], identity=ident[0:64, 0:64]), [yb, ident], [pt])
                          cast(c0 // 16, yst, yst[:, c0:c0 + cn, :], pt, pt[:, 0:cn * 64].rearrange("c (k t) -> c k t", t=64))
                      dma3(pool, YST, YST[:, to:to + 64].rearrange("(ct p) t -> p ct t", p=128), yst, yst[:], sem_buf=yst)

        with (K.phase() if "P4" in enabled else _Skip()) as _ph:
          if _ph is not None:
              TB4 = min(2, TGB); TG4 = TB4 * 128
              NXT = SW // 128; NFT = DFF // 128; NCW = 256
              ident32 = K.sb("ident32", [128, 128], F32); ident = K.sb("ident", [128, 128], BF16)
              K.dma(sp, ident32, ident32[:], c_ident, c_ident[:]); cast(1, ident, ident[:], ident32, ident32[:])
              gffn = K.sb("gffn", [128, D], F32); gfin = K.sb("gfin", [128, D], F32)
              K.dma(sp, gffn, gffn[:], g_ffn, g_ffn[0:1, :].partition_broadcast(128))
              K.dma(sp, gfin, gfin[:], g_fin, g_fin[0:1, :].partition_broadcast(128))
              WFL = max(NFT, NXT, KT, NH) * NCW
              wfl = Ring([K.sb("wfl", [128, WFL], BF16) for _ in range(2)])
              yaT = K.sb("yaT", [128, NH, TG4], BF16); ysT = K.sb("ysT", [128, NXT, TG4], BF16)
              mrg = K.sb("mrg", [128, TB4, D], F32); mrg16 = K.sb("mrg16", [128, TB4, D], BF16)
              mT = K.sb("mT", [128, KT, TG4], BF16)
              xb4 = [K.sb("xb4", [128, D], F32) for _ in range(TB4)]
              hb = K.sb("hb", [128, TB4, D], F32)
              u2 = Ring([K.sb("u2", [128, D], BF16) for _ in range(1)])
              u2T = K.sb("u2T", [128, KT, TG4], BF16)
              actT = K.sb("actT", [128, NFT, TG4], BF16)
              gt_r = Ring([K.sb("gt", [128, NCW], F32) for _ in range(3)])
              tm_r = Ring([K.sb("tm", [128, NCW], F32) for _ in range(3)])
              sg_r = Ring([K.sb("sg", [128, TG4], F32) for _ in range(2)])
              junk4 = K.sb("junk4", [128, D], BF16)
              st4 = Ring([K.sb("st4", [128, 4], F32) for _ in range(2)])
              ps4 = Ring([K.ps("ps4", [128, 512], F32) for _ in range(6)])
              psT4 = Ring([K.ps("psT4", [128, 1024], BF16) for _ in range(2)])

              def loadw4(Wd, nk, col0, n):
                  w = wfl.next()
                  wv = w[:, 0:nk * n].rearrange("p (k n) -> p k n", n=n)
                  dma3(sp, w, wv, Wd, Wd[:, col0:col0 + n].rearrange("(kt p) n -> p kt n", p=128))
                  return w, wv

              def transpose_to(srcb, src_ap_fn, dstT, b):
                  for k0 in range(0, KT, 8):
                      kn = min(8, KT - k0)
                      pt = psT4.next()
                      for kk in range(kn):
                          K.op(pe, lambda: nc.tensor.transpose(out=pt[:, kk * 128:(kk + 1) * 128], in_=src_ap_fn(k0 + kk), identity=ident[:]), [srcb, ident], [pt])
                      cast(k0 // 8, dstT, dstT[:, k0:k0 + kn, b * 128:(b + 1) * 128], pt, pt[:, 0:kn * 128].rearrange("p (k t) -> p k t", t=128))

              def rms_to(hsrc_ap, hsrc_b, gvec, out_b, out_ap):
                  st = st4.next()
                  K.op(act, lambda: nc.scalar.activation(out=junk4[:], in_=hsrc_ap, func=AF.Square, accum_out=st[:, 0:1]), [hsrc_b], [junk4, st])
                  K.op(act, lambda: nc.scalar.activation(out=st[:, 1:2], in_=st[:, 0:1], func=AF.Sqrt, scale=1.0 / D, bias=1e-6), [st], [st])
                  K.op(dve, lambda: nc.vector.reciprocal(out=st[:, 2:3], in_=st[:, 1:2]), [st], [st])
                  K.op(dve, lambda: nc.vector.scalar_tensor_tensor(out=out_ap, in0=hsrc_ap, scalar=st[:, 2:3], in1=gvec[:], op0=ALU.mult, op1=ALU.mult), [hsrc_b, st, gvec], [out_b])

              for g4 in range(T // TG4):
                  to0 = g4 * TG4
                  K.dma(sp, yaT, yaT[:], YAT, YAT[:, to0:to0 + TG4].rearrange("(h d) t -> d h t", d=128))
                  dma3(sp, ysT, ysT[:], YST, YST[:, to0:to0 + TG4].rearrange("(c p) t -> p c t", p=128))
                  for b in range(TB4):
                      K.dma(sp, xb4[b], xb4[b][:], xw, xw[OWN0 + to0 + b * 128:OWN0 + to0 + (b + 1) * 128, :])
                  for n0 in range(0, D, NCW):
                      wa, wav = loadw4(W_ab, NH, n0, NCW)
                      ws, wsv = loadw4(W_sb, NXT, n0, NCW)
                      for b in range(TB4):
                          g0 = gt_r.next(); g1 = gt_r.next(); tm = tm_r.next()
                          K.dma(sp, g0, g0[:], GTs, GTs[to0 + b * 128:to0 + (b + 1) * 128, n0:n0 + NCW])
                          K.dma(sp, g1, g1[:], GTs, GTs[to0 + b * 128:to0 + (b + 1) * 128, D + n0:D + n0 + NCW])
                          pa = ps4.next(); pS = ps4.next()
                          mm(pa, pa[:, 0:NCW], [(yaT, yaT[:, k, b * 128:(b + 1) * 128], wa, wav[:, k, :]) for k in range(NH)])
                          mm(pS, pS[:, 0:NCW], [(ysT, ysT[:, k, b * 128:(b + 1) * 128], ws, wsv[:, k, :]) for k in range(NXT)])
                          K.op(dve, lambda: nc.vector.tensor_tensor(out=tm[:], in0=pa[:, 0:NCW], in1=g0[:], op=ALU.mult), [pa, g0], [tm])
                          K.op(dve, lambda: nc.vector.tensor_tensor(out=mrg[:, b, n0:n0 + NCW], in0=pS[:, 0:NCW], in1=g1[:], op=ALU.mult), [pS, g1], [mrg])
                          K.op(pool, lambda: nc.gpsimd.tensor_tensor(out=mrg16[:, b, n0:n0 + NCW], in0=mrg[:, b, n0:n0 + NCW], in1=tm[:], op=ALU.add), [mrg, tm], [mrg16])
                  for b in range(TB4):
                      transpose_to(mrg16, lambda k: mrg16[:, b, k * 128:(k + 1) * 128], mT, b)
                  for n0 in range(0, D, NCW):
                      w, wv = loadw4(W_out, KT, n0, NCW)
                      for b in range(TB4):
                          p_ = ps4.next()
                          mm(p_, p_[:, 0:NCW], [(mT, mT[:, k, b * 128:(b + 1) * 128], w, wv[:, k, :]) for k in range(KT)])
                          K.op(dve, lambda: nc.vector.tensor_tensor(out=hb[:, b, n0:n0 + NCW], in0=p_[:, 0:NCW], in1=xb4[b][:, n0:n0 + NCW], op=ALU.add), [p_, xb4[b]], [hb])
                  for b in range(TB4):
                      u = u2.next()
                      rms_to(hb[:, b, :], hb, gffn, u, u[:])
                      transpose_to(u, lambda k: u[:, k * 128:(k + 1) * 128], u2T, b)
                  for ft in range(NFT):
                      wg, wgv = loadw4(W_f1, KT, ft * 128, 128)
                      wu, wuv = loadw4(W_f1, KT, DFF + ft * 128, 128)
                      pg = ps4.next(); pu = ps4.next(); sg = sg_r.next()
                      mm(pg, pg[:, 0:TG4], [(wg, wgv[:, k, :], u2T, u2T[:, k, :]) for k in range(KT)])
                      mm(pu, pu[:, 0:TG4], [(wu, wuv[:, k, :], u2T, u2T[:, k, :]) for k in range(KT)])
                      K.op(act, lambda: nc.scalar.activation(out=sg[:], in_=pg[:, 0:TG4], func=AF.Silu), [pg], [sg])
                      K.op(dve, lambda: nc.vector.tensor_tensor(out=actT[:, ft, :], in0=pu[:, 0:TG4], in1=sg[:], op=ALU.mult), [pu, sg], [actT])
                  for n0 in range(0, D, NCW):
                      w, wv = loadw4(W_f2, NFT, n0, NCW)
                      for b in range(TB4):
                          p_ = ps4.next()
                          mm(p_, p_[:, 0:NCW], [(actT, actT[:, k, b * 128:(b + 1) * 128], w, wv[:, k, :]) for k in range(NFT)])
                          K.op(dve, lambda: nc.vector.tensor_tensor(out=hb[:, b, n0:n0 + NCW], in0=p_[:, 0:NCW], in1=hb[:, b, n0:n0 + NCW], op=ALU.add), [p_, hb], [hb])
                  for b in range(TB4):
                      rms_to(hb[:, b, :], hb, gfin, mrg, mrg[:, b, :])
                      K.dma(pool, out, out[to0 + b * 128:to0 + (b + 1) * 128, :], mrg, mrg[:, b, :], sem_buf=mrg)

        K.barrier()
    return nc, C


def rope_tab(pos, dim):
    inv = (10000.0 ** (-np.arange(0, dim, 2, dtype=np.float32) / np.float32(dim))).astype(np.float32)
    ang = pos.astype(np.float32)[:, None] * inv[None, :]
    return np.cos(ang).astype(np.float32), np.sin(ang).astype(np.float32)


def consts():
    c = {}
    c["c_ident"] = np.eye(128, dtype=np.float32)
    p = np.zeros((128, 128), np.float32)
    for m in range(64):
        p[m + 64, m] = -1.0; p[m, m + 64] = 1.0
    c["c_pt128"] = p
    p = np.zeros((64, 64), np.float32)
    for m in range(32):
        p[m + 32, m] = -1.0; p[m, m + 32] = 1.0
    c["c_pt64"] = p
    t = np.arange(128)[:, None]; s = np.arange(128)[None, :]
    c["c_dmask"] = np.where(s < (t // 64 + 1) * 64, 0.0, NEG).astype(np.float32)
    j = np.arange(64)[:, None]; t = np.arange(64)[None, :]
    c["c_umat"] = (j <= t).astype(np.float32)
    c["c_lmat"] = (j > t).astype(np.float32)
    c["c_caus"] = (j <= t).astype(np.float32)
    return c


def make_in_maps(C, inputs, ncores=8):
    C = derive(C)
    S, D, T, WIN = C["S"], C["D"], C["T"], C["WIN"]
    f = lambda a: np.ascontiguousarray(np.asarray(a, dtype=np.float32))
    shared = dict(
        w_in=f(inputs["w_in"][0]), w_gate=f(inputs["w_gate"][0]), w_ab=f(inputs["w_attn_branch"][0]),
        w_sb=f(inputs["w_ssm_branch"][0]), w_out=f(inputs["w_out"][0]), conv_w=f(inputs["conv_w"][0]),
        conv_b=f(inputs["conv_b"]), dt_bias=f(inputs["dt_bias"]), a_log=f(inputs["a_log"]), d_skip=f(inputs["d_skip"]),
        g_ssm=f(inputs["g_ssm_norm"]), g_mix=f(inputs["g_mix"]), g_ffn=f(inputs["g_ffn"]),
        w_f1=f(inputs["w_ffn_in"][0]), w_f2=f(inputs["w_ffn_out"][0]), g_fin=f(inputs["g_final"]).reshape(1, D),
    )
    shared.update(consts())
    x = np.asarray(inputs["x"], dtype=np.float32)
    maps = []
    for c in range(ncores):
        b, r = c // 4, c % 4
        end = (r + 1) * T
        npad = WIN - end
        xwin = np.zeros((WIN, D), np.float32)
        xwin[npad:] = x[b, :end]
        pos = np.arange(WIN) - npad
        vmask = (pos >= 0)
        ca_, sa_ = rope_tab(np.maximum(pos, 0), 128)
        ci_, si_ = rope_tab(np.maximum(pos, 0), 64)
        m = dict(shared)
        m["xw"] = xwin
        m["valid"] = vmask.astype(np.float32).reshape(WIN, 1)
        m["kbias"] = np.where(vmask, 0.0, NEG).astype(np.float32).reshape(1, WIN)
        m["ca"] = np.ascontiguousarray(np.concatenate([ca_.T, ca_.T], 0)); m["sa"] = np.ascontiguousarray(np.concatenate([sa_.T, sa_.T], 0))
        m["ci"] = np.ascontiguousarray(np.concatenate([ci_.T, ci_.T], 0)); m["si"] = np.ascontiguousarray(np.concatenate([si_.T, si_.T], 0))
        maps.append(m)
    return maps


_CACHE = {}


def kernel(**inputs):
    C = derive(REAL)
    if "nc" not in _CACHE:
        _CACHE["nc"] = build(REAL)[0]
    nc = _CACHE["nc"]
    maps = make_in_maps(REAL, inputs)
    res = run_bass_kernel_spmd(nc, maps, core_ids=list(range(8)))
    T = C["T"]
    out = np.zeros((2, C["S"], C["D"]), np.float32)
    for c in range(8):
        b, r = c // 4, c % 4
        out[b, r * T:(r + 1) * T] = res.results[c]["out"]
    return out
```
